# Optimizing a Trainium2 kernel written in Bass

```python
import jax, jax.numpy as jnp
from jax import lax
import numpy as np

D_MODEL = 2048
BATCH = 1
SEQ = 16384
DEPTH = 2

HEAD_DIM = 128
CONV_CH = 512
CONV_WIDTH = 31
CONV_HALF = CONV_WIDTH // 2
B_Q_HEADS = 8
B_KV_HEADS = 2
B_GROUP = B_Q_HEADS // B_KV_HEADS
B_HALF_WINDOW = 128
C_PATTERNS = ((128, 1), (512, 4), (2048, 16))
C_GROUPS = len(C_PATTERNS)
C_HEADS_PER_GROUP = 4
C_HEADS = C_GROUPS * C_HEADS_PER_GROUP
N_BRANCHES = 3
N_ATTN_HEADS = B_Q_HEADS + C_HEADS
ALIBI_MAX_EXP = 8.0
D_FF = -(-8 * D_MODEL // (3 * 256)) * 256
EPS = 1e-6
NEG = -1e30

A_IN_W = 2 * CONV_CH
BQ_W = B_Q_HEADS * HEAD_DIM
BKV_W = B_KV_HEADS * HEAD_DIM
CQKV_W = C_HEADS * HEAD_DIM
GATE_W = N_BRANCHES * D_MODEL
IN_SPLITS = (A_IN_W, BQ_W, BKV_W, BKV_W, CQKV_W, CQKV_W, CQKV_W, GATE_W)
IN_WIDTH = sum(IN_SPLITS)
IN_SPLIT_IDX = tuple(int(v) for v in np.cumsum(IN_SPLITS)[:-1])
B_OUT_W = B_Q_HEADS * HEAD_DIM
C_OUT_W = C_HEADS_PER_GROUP * HEAD_DIM

kernel_name = "hybrid_conv_swa_dilated_gated_encoder"


def rmsnorm(x, g):
    xf = x.astype(jnp.float32)
    y = xf * lax.rsqrt(jnp.mean(xf * xf, axis=-1, keepdims=True) + EPS)
    return (y * g.astype(jnp.float32)).astype(x.dtype)


def layernorm(x, g, b):
    xf = x.astype(jnp.float32)
    mu = jnp.mean(xf, axis=-1, keepdims=True)
    var = jnp.mean(jnp.square(xf - mu), axis=-1, keepdims=True)
    y = (xf - mu) * lax.rsqrt(var + EPS)
    return (y * g.astype(jnp.float32) + b.astype(jnp.float32)).astype(x.dtype)


def alibi_slopes():
    h = jnp.arange(1, N_ATTN_HEADS + 1, dtype=jnp.float32)
    return jnp.exp2(-ALIBI_MAX_EXP * h / N_ATTN_HEADS)


def banded_attention(q, k, v, slopes, half, step, n_valid, sink=None):
    n, length, hk, g, dh = q.shape
    blk = half
    nb = length // blk
    qb = q.reshape(n, nb, blk, hk, g, dh)

    def windows(t):
        pad = jnp.zeros((n, blk, hk, dh), t.dtype)
        tp = jnp.concatenate([pad, t, pad], axis=1).reshape(n, nb + 2, blk, hk, dh)
        return jnp.concatenate([tp[:, :-2], tp[:, 1:-1], tp[:, 2:]], axis=2)

    kw, vw = windows(k), windows(v)
    s = jnp.einsum("nbqhgd,nbkhd->nbhgqk", qb, kw).astype(jnp.float32) * (dh ** -0.5)
    qi = jnp.arange(blk)
    kj = jnp.arange(3 * blk)
    rel = kj[None, :] - blk - qi[:, None]
    k_pos = (jnp.arange(nb)[:, None] - 1) * blk + kj[None, :]
    valid = (jnp.abs(rel) <= half)[None] & ((k_pos >= 0) & (k_pos < n_valid))[:, None, :]
    dist = (step * jnp.abs(rel)).astype(jnp.float32)
    bias = -slopes.astype(jnp.float32)[:, :, None, None] * dist[None, None]
    s = jnp.where(valid[None, :, None, None], s + bias, NEG)
    m = jnp.max(s, axis=-1, keepdims=True)
    if sink is not None:
        sink_b = sink.astype(jnp.float32)[:, :, None, None]
        m = jnp.maximum(m, sink_b)
    p = jnp.exp(s - m)
    denom = jnp.sum(p, axis=-1, keepdims=True)
    if sink is not None:
        denom = denom + jnp.exp(sink_b - m)
    o = jnp.einsum("nbhgqk,nbkhd->nbqhgd", p.astype(v.dtype), vw).astype(jnp.float32)
    o = o / jnp.transpose(denom, (0, 1, 4, 2, 3, 5))
    lse = jnp.transpose((m + jnp.log(denom))[..., 0], (0, 1, 4, 2, 3)).reshape(n, length, hk, g)
    return o.reshape(n, length, hk, g, dh).astype(q.dtype), lse


def conv_module(u, conv_w, conv_b, cnorm_g, cnorm_b):
    a, gt = jnp.split(u, 2, axis=-1)
    z = a * jax.nn.sigmoid(gt)
    z = lax.conv_general_dilated(
        z, conv_w[:, None, :], window_strides=(1,), padding=[(CONV_HALF, CONV_HALF)],
        dimension_numbers=("NWC", "WIO", "NWC"), feature_group_count=CONV_CH)
    z = z + conv_b
    return jax.nn.silu(layernorm(z, cnorm_g, cnorm_b))


def windowed_gqa(q, k, v, sink, slopes):
    b, s, _ = q.shape
    q = q.reshape(b, s, B_KV_HEADS, B_GROUP, HEAD_DIM)
    k = k.reshape(b, s, B_KV_HEADS, HEAD_DIM)
    v = v.reshape(b, s, B_KV_HEADS, HEAD_DIM)
    o, _ = banded_attention(q, k, v, slopes.reshape(B_KV_HEADS, B_GROUP), B_HALF_WINDOW, 1, s,
                            sink=sink.reshape(B_KV_HEADS, B_GROUP))
    return o.reshape(b, s, B_OUT_W)


def to_lattice(t, r, length, padded):
    b, s, h, dh = t.shape
    t = t.reshape(b, length, r, h, dh).transpose(0, 2, 1, 3, 4).reshape(b * r, length, h, dh)
    return jnp.pad(t, ((0, 0), (0, padded - length), (0, 0), (0, 0)))


def dilated_attention(q, k, v, slopes):
    b, s, _ = q.shape
    q = q.reshape(b, s, C_HEADS, HEAD_DIM)
    k = k.reshape(b, s, C_HEADS, HEAD_DIM)
    v = v.reshape(b, s, C_HEADS, HEAD_DIM)
    outs, lses = [], []
    for gi, (w, r) in enumerate(C_PATTERNS):
        half = w // (2 * r)
        length = s // r
        padded = -(-length // half) * half
        hs = slice(gi * C_HEADS_PER_GROUP, (gi + 1) * C_HEADS_PER_GROUP)
        o, lse = banded_attention(
            to_lattice(q[:, :, hs], r, length, padded)[:, :, :, None],
            to_lattice(k[:, :, hs], r, length, padded),
            to_lattice(v[:, :, hs], r, length, padded),
            slopes[hs][:, None], half, r, length)
        o = o[:, :length, :, 0].reshape(b, r, length, C_HEADS_PER_GROUP, HEAD_DIM)
        outs.append(o.transpose(0, 2, 1, 3, 4).reshape(b, s, C_HEADS_PER_GROUP, HEAD_DIM))
        lse = lse[:, :length, :, 0].reshape(b, r, length, C_HEADS_PER_GROUP)
        lses.append(lse.transpose(0, 2, 1, 3).reshape(b, s, C_HEADS_PER_GROUP))
    alpha = jax.nn.softmax(jnp.stack(lses), axis=0)
    out = jnp.einsum("gbsh,gbshd->bshd", alpha.astype(q.dtype), jnp.stack(outs))
    return out.reshape(b, s, C_OUT_W)


def hybrid_layer(x, ln1_g, w_in, conv_w, conv_b, cnorm_g, cnorm_b, w_a, sink, w_b, w_c, w_o,
                 ln2_g, w_ffn_in, w_ffn_out):
    b, s, d = x.shape
    h = rmsnorm(x, ln1_g)
    proj = h @ w_in
    u_a, q_b, k_b, v_b, q_c, k_c, v_c, gate_logits = jnp.split(proj, IN_SPLIT_IDX, axis=-1)
    slopes = alibi_slopes()
    y_a = conv_module(u_a, conv_w, conv_b, cnorm_g, cnorm_b) @ w_a
    y_b = windowed_gqa(q_b, k_b, v_b, sink, slopes[:B_Q_HEADS]) @ w_b
    y_c = dilated_attention(q_c, k_c, v_c, slopes[B_Q_HEADS:]) @ w_c
    gates = jax.nn.sigmoid(gate_logits.astype(jnp.float32)).astype(x.dtype).reshape(b, s, N_BRANCHES, d)
    mixed = gates[:, :, 0] * y_a + gates[:, :, 1] * y_b + gates[:, :, 2] * y_c
    x = x + mixed @ w_o
    h2 = rmsnorm(x, ln2_g)
    g_ff, u_ff = jnp.split(h2 @ w_ffn_in, 2, axis=-1)
    return x + (jax.nn.silu(g_ff) * u_ff) @ w_ffn_out


def setup_inputs(seed: int = 0) -> dict:
    key = jax.random.key(seed)
    ks = jax.random.split(key, 17)
    f32 = jnp.float32

    def nrm(k, shape, scale):
        return jax.random.normal(k, shape, f32) * scale

    return {
        "x": nrm(ks[0], (BATCH, SEQ, D_MODEL), 1.0),
        "ln1_g": 1.0 + nrm(ks[1], (DEPTH, D_MODEL), 0.02),
        "w_in": nrm(ks[2], (DEPTH, D_MODEL, IN_WIDTH), D_MODEL ** -0.5),
        "conv_w": nrm(ks[3], (DEPTH, CONV_WIDTH, CONV_CH), CONV_WIDTH ** -0.5),
        "conv_b": nrm(ks[4], (DEPTH, CONV_CH), 0.02),
        "cnorm_g": 1.0 + nrm(ks[5], (DEPTH, CONV_CH), 0.02),
        "cnorm_b": nrm(ks[6], (DEPTH, CONV_CH), 0.02),
        "w_a": nrm(ks[7], (DEPTH, CONV_CH, D_MODEL), CONV_CH ** -0.5),
        "sink": nrm(ks[8], (DEPTH, B_Q_HEADS), 0.5),
        "w_b": nrm(ks[9], (DEPTH, B_OUT_W, D_MODEL), B_OUT_W ** -0.5),
        "w_c": nrm(ks[10], (DEPTH, C_OUT_W, D_MODEL), C_OUT_W ** -0.5),
        "w_o": nrm(ks[11], (DEPTH, D_MODEL, D_MODEL), D_MODEL ** -0.5),
        "ln2_g": 1.0 + nrm(ks[12], (DEPTH, D_MODEL), 0.02),
        "w_ffn_in": nrm(ks[13], (DEPTH, D_MODEL, 2 * D_FF), D_MODEL ** -0.5),
        "w_ffn_out": nrm(ks[14], (DEPTH, D_FF, D_MODEL), D_FF ** -0.5),
        "lnf_g": 1.0 + nrm(ks[15], (D_MODEL,), 0.02),
    }


def reference(x, ln1_g, w_in, conv_w, conv_b, cnorm_g, cnorm_b, w_a, sink, w_b, w_c, w_o,
              ln2_g, w_ffn_in, w_ffn_out, lnf_g):
    for l in range(DEPTH):
        x = hybrid_layer(x, ln1_g[l], w_in[l], conv_w[l], conv_b[l], cnorm_g[l], cnorm_b[l],
                         w_a[l], sink[l], w_b[l], w_c[l], w_o[l], ln2_g[l], w_ffn_in[l],
                         w_ffn_out[l])
    return rmsnorm(x, lnf_g)
```

```python
import contextlib
import numpy as np
import concourse.bass as bass
import concourse.mybir as mybir
from concourse.bass_utils import run_bass_kernel_spmd

F32 = mybir.dt.float32
BF16 = mybir.dt.bfloat16
I32 = mybir.dt.int32
AF = mybir.ActivationFunctionType
ALU = mybir.AluOpType

NCORES = 8
SEQ = 16384
D = 2048
T = 2048
DFF = 5632
INW = 13312
EPS = 1e-6
SCALE = float(128 ** -0.5)
NEGM = -30000.0
ENGS = ("pe", "act", "dve", "pool", "sp")
SEM_ROLL = 20000


class Slot:
    def __init__(self, name):
        self.name = name
        self.count = 0


class Sched:
    def __init__(self):
        self.ops = {e: [] for e in ENGS}
        self.cnt = {e: 0 for e in ENGS}
        self.epoch = {e: 0 for e in ENGS}
        self.pending = {e: False for e in ENGS}
        self.last_w = {}
        self.readers = {}
        self.seen = {e: {} for e in ENGS}
        self.semkeys = []
        self._semset = set()
        self.slots = {}
        self.bank_rr = 0

    def slot(self, name):
        if name not in self.slots:
            self.slots[name] = Slot(name)
        return self.slots[name]

    def _semkey(self, k):
        if k not in self._semset:
            self._semset.add(k)
            self.semkeys.append(k)
        return k

    def _eng_key(self, e):
        return self._semkey(("E", e, self.epoch[e]))

    def _mk_waits(self, eng, toks):
        waits = {}
        for (k, v) in toks:
            if eng == "pe" and k[0] == "E" and k[1] == "pe":
                continue
            if self.seen[eng].get(k, -1) >= v:
                continue
            if waits.get(k, -1) < v:
                waits[k] = v
        for k, v in waits.items():
            self.seen[eng][k] = v
        return list(waits.items())

    def _collect(self, eng, reads, writes):
        toks = []
        for r in reads:
            t = self.last_w.get(r)
            if t is not None:
                toks.append(t)
        for w in writes:
            t = self.last_w.get(w)
            if t is not None:
                toks.append(t)
            toks.extend(self.readers.get(w, ()))
        return self._mk_waits(eng, toks)

    def _record(self, tok, reads, writes):
        for r in reads:
            self.readers.setdefault(r, []).append(tok)
        for w in writes:
            self.last_w[w] = tok
            self.readers[w] = []

    def op(self, eng, fn, reads=(), writes=(), inc=True):
        waits = self._collect(eng, reads, writes)
        if self.cnt[eng] >= SEM_ROLL and not self.pending[eng]:
            self.epoch[eng] += 1
            self.cnt[eng] = 0
        key = self._eng_key(eng)
        if inc:
            self.cnt[eng] += 1
            tok = (key, self.cnt[eng])
            incs = [(key, 1)]
            self.pending[eng] = False
        else:
            tok = (key, self.cnt[eng] + 1)
            incs = []
            self.pending[eng] = True
        self.ops[eng].append((waits, fn, incs))
        self._record(tok, reads, writes)
        return tok

    def dma(self, queue, slotname, fn, reads=(), writes=(), self_wait=False):
        slot = self.slot(slotname)
        waits = self._collect(queue, reads, writes)
        key = self._semkey(("D", slot.name))
        if self_wait and slot.count:
            waits = waits + self._mk_waits(queue, [(key, slot.count)])
        slot.count += 16
        tok = (key, slot.count)
        self.ops[queue].append((waits, fn, [(key, 16)]))
        self._record(tok, reads, writes)
        return tok

    def all_tokens(self):
        toks = []
        for e in ENGS:
            assert not self.pending[e]
            if self.cnt[e] > 0:
                toks.append((("E", e, self.epoch[e]), self.cnt[e]))
        for s in self.slots.values():
            if s.count:
                toks.append((("D", s.name), s.count))
        return toks

    def barrier(self):
        toks = self.all_tokens()
        for e in ENGS:
            w = self._mk_waits(e, toks)
            if w:
                self.ops[e].append((w, None, []))
        self.last_w = {}
        self.readers = {}

    def emit(self, nc, stack):
        for e in ENGS:
            assert not self.pending[e], e
        sems = {}
        for i, k in enumerate(self.semkeys):
            sems[k] = stack.enter_context(nc.semaphore("s%d" % i))
        ops = self.ops

        def run(eng_name):
            def body(eng):
                for (waits, fn, incs) in ops[eng_name]:
                    for (k, v) in waits:
                        eng.wait_ge(sems[k], v)
                    if fn is None:
                        continue
                    try:
                        ins = fn(eng)
                    except Exception:
                        print("EMIT FAIL", fn.__defaults__, eng_name, "op#", ops[eng_name].index((waits, fn, incs)), "of", len(ops[eng_name]))
                        raise
                    for (k, n) in incs:
                        ins.then_inc(sems[k], n)
            return body

        with nc.Block() as block:
            block.tensor(run("pe"))
            block.scalar(run("act"))
            block.vector(run("dve"))
            block.gpsimd(run("pool"))
            block.sync(run("sp"))


class Arena:
    def __init__(self, t, nwords):
        self.t = t
        self.n = nwords
        self.off = 0

    def f32(self, n):
        assert self.off + n <= self.n, ("arena overflow", self.off, n, self.n)
        ap = self.t[:, self.off:self.off + n]
        self.off += n
        return ap

    def bf16(self, n):
        w = (n + 1) // 2
        assert self.off + w <= self.n, ("arena overflow", self.off, w, self.n)
        ap = self.t[:, self.off:self.off + w].bitcast(BF16)
        self.off += w
        return ap

    def i32(self, n):
        ap = self.t[:, self.off:self.off + n].bitcast(I32)
        self.off += n
        return ap


class Builder:
    def __init__(self, nc, S, st, arena_words):
        self.nc = nc
        self.S = S
        sb = st.enter_context(nc.sbuf_tensor("arena", [128, arena_words], F32))
        self.A = Arena(sb, arena_words)
        self.banks = [st.enter_context(nc.psum_tensor("ps%d" % i, [128, 512], F32)) for i in range(8)]
        self.uid = 0

    def bank(self):
        i = self.S.bank_rr % 8
        self.S.bank_rr += 1
        return i

    def mm(self, out, lhsT, rhs, start, stop, reads, writes, inc):
        self.S.op("pe", lambda e: e.matmul(out=out, lhsT=lhsT, rhs=rhs, start=start, stop=stop),
                  reads=reads, writes=writes, inc=inc)

    def consts(self):
        S, A = self.S, self.A
        self.identf = A.f32(128)
        self.onesf = A.f32(128)
        self.zerosf = A.f32(128)
        self.ident = A.bf16(128)
        self.ones = A.bf16(128)
        self.zcol = A.f32(1)
        S.op("pool", lambda e: e.memset(self.identf, 0.0), writes=["identf"])
        S.op("pool", lambda e: e.affine_select(out=self.identf, in_=self.identf, pattern=[[-1, 128]],
                                               compare_op=ALU.not_equal, fill=1.0, base=0,
                                               channel_multiplier=1), reads=["identf"], writes=["identf"])
        S.op("pool", lambda e: e.memset(self.onesf, 1.0), writes=["onesf"])
        S.op("pool", lambda e: e.memset(self.zerosf, 0.0), writes=["zerosf"])
        S.op("pool", lambda e: e.memset(self.zcol, 0.0), writes=["zcol"])
        S.op("dve", lambda e: e.tensor_copy(out=self.ident, in_=self.identf), reads=["identf"], writes=["ident"])
        S.op("dve", lambda e: e.tensor_copy(out=self.ones, in_=self.onesf), reads=["onesf"], writes=["ones"])

    def norm_tile(self, x_ap, xkey, g_ap, gkey, hb_ap, hbkey, junk_ap, ss_ap, tag):
        S = self.S
        sk = "ss" + tag
        S.op("act", lambda e: e.activation(out=junk_ap, in_=x_ap, func=AF.Square, accum_out=ss_ap[:, 0:1]),
             reads=[xkey], writes=["junk" + tag, sk])
        S.op("act", lambda e: e.activation(out=ss_ap[:, 1:2], in_=ss_ap[:, 0:1], func=AF.Sqrt,
                                           scale=1.0 / D, bias=EPS), reads=[sk], writes=[sk + "b"])
        S.op("dve", lambda e: e.reciprocal(out=ss_ap[:, 2:3], in_=ss_ap[:, 1:2]), reads=[sk + "b"], writes=[sk + "c"])
        S.op("dve", lambda e: e.scalar_tensor_tensor(out=hb_ap, in0=x_ap, scalar=ss_ap[:, 2:3], in1=g_ap,
                                                     op0=ALU.mult, op1=ALU.mult),
             reads=[xkey, sk + "c", gkey], writes=[hbkey])

    def transpose_tile(self, hb_ap, hbkey, dst_fn, dstkeys, flip):
        S = self.S
        for half in range(2):
            b = self.bank()
            pt = self.banks[b][:].bitcast(BF16).rearrange("p (a c) -> p a c", c=128)
            for j in range(8):
                kc = half * 8 + j
                S.op("pe", lambda e, j=j, kc=kc, pt=pt: e.transpose(out=pt[:, j, :], in_=hb_ap[:, kc * 128:(kc + 1) * 128],
                                                                    identity=self.ident),
                     reads=[hbkey, "ident"], writes=[("ps", b)], inc=(j == 7))
            dst = dst_fn(half)
            if (half + flip) % 2 == 0:
                S.op("dve", lambda e, dst=dst, pt=pt: e.tensor_copy(out=dst, in_=pt), reads=[("ps", b)], writes=dstkeys)
            else:
                S.op("act", lambda e, dst=dst, pt=pt: e.activation(out=dst, in_=pt, func=AF.Copy),
                     reads=[("ps", b)], writes=dstkeys)


def build_program(layer_ids, final, debug=False, stop=None):
    nc = bass.Bass("TRN2", target_bir_lowering=False)
    dt = {}

    def din(name, shape, dtype=F32):
        dt[name] = nc.dram_tensor(name, list(shape), dtype, kind="ExternalInput")
        return dt[name]

    xown = din("xown", [T, D])
    xhalo = din("xhalo", [T, D])
    idx_d = din("idx", [128, 32], I32)
    kmask_d = din("kmask", [128, 4])
    biasB_d = din("biasB", [128, 3072])
    biasC_d = din("biasC", [128, 3072])
    ln1_g = din("ln1_g", [2, D]); w_in = din("w_in", [2, D, INW]); conv_w = din("conv_w", [2, 31, 512])
    conv_b = din("conv_b", [2, 512]); cnorm_g = din("cnorm_g", [2, 512]); cnorm_b = din("cnorm_b", [2, 512])
    w_a = din("w_a", [2, 512, D]); sink = din("sink", [2, 8]); w_b = din("w_b", [2, 1024, D])
    w_c = din("w_c", [2, 512, D]); w_o = din("w_o", [2, D, D]); ln2_g = din("ln2_g", [2, D])
    w_ffn_in = din("w_ffn_in", [2, D, 2 * DFF]); w_ffn_out = din("w_ffn_out", [2, DFF, D]); lnf_g = din("lnf_g", [D])
    xout = nc.dram_tensor("xout", [T, D], F32, kind="ExternalOutput")

    S = Sched()
    ARENA_WORDS = 52500
    with contextlib.ExitStack() as st:
        B = Builder(nc, S, st, ARENA_WORDS)
        A = B.A
        B.consts()
        idx_sb = A.i32(32)
        kmask = A.f32(4)
        S.dma("sp", "c_idx", lambda e: e.dma_start(out=idx_sb, in_=idx_d.ap()), writes=["idx"])
        S.dma("sp", "c_km", lambda e: e.dma_start(out=kmask, in_=kmask_d.ap()), writes=["kmask"])
        base_off = A.off
        S.barrier()
        for li, l in enumerate(layer_ids):
            def halo_src(e_t):
                if 8 <= e_t < 24:
                    return "static", xown.ap()[(e_t - 8) * 128:(e_t - 7) * 128, :]
                hi = e_t if e_t < 8 else e_t - 16
                return "static", xhalo.ap()[hi * 128:(hi + 1) * 128, :]
            emit_layer(nc, B, l, xown, halo_src, xout, final and li == len(layer_ids) - 1, dict(
                idx=idx_sb, kmask=kmask, biasB=biasB_d, biasC=biasC_d, ln1_g=ln1_g, w_in=w_in, conv_w=conv_w,
                conv_b=conv_b, cnorm_g=cnorm_g, cnorm_b=cnorm_b, w_a=w_a, sink=sink, w_b=w_b, w_c=w_c, w_o=w_o,
                ln2_g=ln2_g, w_ffn_in=w_ffn_in, w_ffn_out=w_ffn_out, lnf_g=lnf_g), base_off, debug, stop)
        S.barrier()
        S.emit(nc, st)
    return nc


def bcast_row(t, off, n):
    return bass.AP(t, off, [[0, 128], [1, n]])


def emit_layer(nc, B, l, x_own, halo_src, x_dst, final, W, base_off, debug, stop=None):
    S, A = B.S, B.A
    L = "L%d" % l

    def scr(name, shape, dtype=BF16):
        if debug:
            return nc.dram_tensor(name + L, list(shape), dtype, kind="ExternalOutput")
        return nc.dram_tensor(name + L, list(shape), dtype)

    UT = scr("UT", [1024, 2304])
    QB = scr("QB", [1024, 2048])
    KB = scr("KB", [256, 2304])
    VB = scr("VB", [2304, 256])
    QC = [scr("QC%d" % g, [512, 2048]) for g in range(3)]
    KCL = [2304, 2560, 4096]
    KC = [scr("KC%d" % g, [512, KCL[g]]) for g in range(3)]
    VC = [scr("VC%d" % g, [KCL[g], 512]) for g in range(3)]
    HT = scr("HT", [D, T])
    MIX = scr("MIX", [D, T])
    RR = [1, 4, 16]

    A.off = base_off
    g1 = A.f32(D)
    xt = [A.f32(D) for _ in range(2)]
    junk = A.bf16(D)
    hb = [A.bf16(D) for _ in range(2)]
    ssb = A.f32(4 * 4)
    NHT = 2560
    hT = A.bf16(16 * NHT).rearrange("p (a t) -> p a t", t=NHT)
    NW = 3
    wt = [A.bf16(16 * 512).rearrange("p (a n) -> p a n", n=512) for _ in range(NW)]
    stg = [A.bf16(NHT) for _ in range(2)]
    vst = [A.bf16(512) for _ in range(2)]
    S.dma("sp", "g1", lambda e: e.dma_start(out=g1, in_=bcast_row(W["ln1_g"], l * D, D)), writes=["g1"])

    wcount = [0]

    def load_wblock(col0, ncols=512):
        k = wcount[0] % NW
        wcount[0] += 1
        src = W["w_in"].ap()[l, :, col0:col0 + ncols].rearrange("(kc p) n -> p kc n", p=128)
        S.dma("pool", "wP%d" % k, lambda e: e.dma_start(out=wt[k][:, :, 0:ncols], in_=src), writes=[("wt", k)])
        return k

    tcount = [0]

    def stage_a(e_tile, col):
        i = tcount[0] % 2
        tcount[0] += 1
        kind, src = halo_src(e_tile)
        if kind == "static":
            S.dma("sp", "xt%d" % i, lambda e, src=src: e.dma_start(out=xt[i], in_=src), writes=[("xt", i)])
        else:
            S.op("pool", lambda e: e.memset(xt[i], 0.0), writes=[("xt", i)])
            S.dma("pool", "xt%d" % i, lambda e, src=src, e_tile=e_tile: e.indirect_dma_start(
                out=xt[i], out_offset=None, in_=src,
                in_offset=bass.IndirectOffsetOnAxis(ap=W["idx"][:, e_tile:e_tile + 1], axis=0),
                bounds_check=SEQ - 1, oob_is_err=False), reads=["idx"], writes=[("xt", i)])
        B.norm_tile(xt[i], ("xt", i), g1, "g1", hb[i], ("hb", i), junk, ssb[:, 4 * i:4 * i + 4], "P%d" % i)
        B.transpose_tile(hb[i], ("hb", i),
                         lambda half: hT[:, half * 8:(half + 1) * 8, col * 128:(col + 1) * 128],
                         [("hT", col)], i)

    def proj_fm(k, ct, t0, t1, dst_dram_fn, deint_r, lbase):
        ntok = (t1 - t0) * 128
        si = B.uid % 2
        B.uid += 1
        sg = stg[si]
        pos = 0
        flip = 0
        while pos < ntok:
            n = min(512, ntok - pos)
            b = B.bank()
            ps = B.banks[b]
            c0 = t0 * 128 + pos
            rk = [("wt", k)] + [("hT", c) for c in range(c0 // 128, (c0 + n) // 128)]
            for kc in range(16):
                B.mm(ps[:, 0:n], wt[k][:, kc, ct * 128:(ct + 1) * 128], hT[:, kc, c0:c0 + n],
                     kc == 0, kc == 15, rk, [("ps", b)], kc == 15)
            if deint_r == 1:
                dst = sg[:, pos:pos + n]
                src = ps[:, 0:n]
            else:
                r = deint_r
                Lr = ntok // r
                dst = sg[:, 0:ntok].rearrange("p (r l) -> p r l", r=r)[:, :, pos // r:(pos + n) // r]
                src = ps[:, 0:n].rearrange("p (l r) -> p r l", r=r)
            if flip % 2 == 0:
                S.op("act", lambda e, dst=dst, src=src: e.activation(out=dst, in_=src, func=AF.Copy),
                     reads=[("ps", b)], writes=[("stg", si)])
            else:
                S.op("dve", lambda e, dst=dst, src=src: e.tensor_copy(out=dst, in_=src),
                     reads=[("ps", b)], writes=[("stg", si)])
            flip += 1
            pos += n
        dst_dram, dkey = dst_dram_fn(sg, ntok)
        S.dma("sp", "stg%d" % si, lambda e: e.dma_start(out=dst_dram[0], in_=dst_dram[1]),
              reads=[("stg", si)], writes=[dkey])

    def proj_tm(k, c_lo, ncols, t0, t1, dram_t, row0, dcol0, dkey):
        for t in range(t0, t1):
            b = B.bank()
            ps = B.banks[b]
            for kc in range(16):
                B.mm(ps[:, 0:ncols], hT[:, kc, t * 128:(t + 1) * 128], wt[k][:, kc, c_lo:c_lo + ncols],
                     kc == 0, kc == 15, [("wt", k), ("hT", t)], [("ps", b)], kc == 15)
            si = B.uid % 2
            B.uid += 1
            v = vst[si][:, 0:ncols]
            if si == 0:
                S.op("act", lambda e, v=v, ps=ps: e.activation(out=v, in_=ps[:, 0:ncols], func=AF.Copy),
                     reads=[("ps", b)], writes=[("vst", si)])
            else:
                S.op("dve", lambda e, v=v, ps=ps: e.tensor_copy(out=v, in_=ps[:, 0:ncols]),
                     reads=[("ps", b)], writes=[("vst", si)])
            r0 = row0 + (t - t0) * 128
            dst = dram_t.ap()[r0:r0 + 128, dcol0:dcol0 + ncols]
            S.dma("sp", "vst%d" % si, lambda e, dst=dst, v=v: e.dma_start(out=dst, in_=v),
                  reads=[("vst", si)], writes=[dkey])

    def fm_dst(dram_t, row0, col0, ntok_total_cols=None, r=1, l0=0):
        def f(sg, ntok):
            if r == 1:
                d = dram_t.ap()[row0:row0 + 128, col0:col0 + ntok]
                s = sg[:, 0:ntok]
            else:
                Lt = dram_t.shape[1] // r
                Lr = ntok // r
                d = dram_t.ap()[row0:row0 + 128, :].rearrange("p (r l) -> p r l", r=r)[:, :, l0:l0 + Lr]
                s = sg[:, 0:ntok].rearrange("p (r l) -> p r l", r=r)
            return (d, s), (dram_t.name, "w")
        return f

    for e_t in range(6, 26):
        stage_a(e_t, e_t - 6)
    S.dma("sp", "htst", lambda e: e.dma_start(out=HT.ap().rearrange("(kc p) t -> p kc t", p=128),
                                               in_=hT[:, :, 256:256 + T]),
          reads=[("hT", c) for c in range(2, 18)], writes=[("HT", "w")])

    def colrange(lo_e, hi_e):
        return lo_e - 6, hi_e - 6

    for blk in range(2):
        k = load_wblock(blk * 512)
        for ct in range(4):
            t0, t1 = colrange(7, 25)
            proj_fm(k, ct, t0, t1, fm_dst(UT, blk * 512 + ct * 128, 0), 1, 0)
    for blk in range(2):
        k = load_wblock(1024 + blk * 512)
        for ct in range(4):
            t0, t1 = colrange(8, 24)
            proj_fm(k, ct, t0, t1, fm_dst(QB, blk * 512 + ct * 128, 0), 1, 0)
    k = load_wblock(2048)
    t0, t1 = colrange(7, 25)
    for ct in range(2):
        proj_fm(k, ct, t0, t1, fm_dst(KB, ct * 128, 0), 1, 0)
    proj_tm(k, 256, 256, t0, t1, VB, 0, 0, ("VB", "w"))
    for g in range(3):
        k = load_wblock(2560 + g * 512)
        t0, t1 = colrange(8, 24)
        for ct in range(4):
            proj_fm(k, ct, t0, t1, fm_dst(QC[g], ct * 128, 0, r=RR[g], l0=0), RR[g], 0)
    kr = [(7, 25), (6, 26), (6, 26)]
    for g in range(3):
        k = load_wblock(4096 + g * 512)
        t0, t1 = colrange(*kr[g])
        l0 = 48 if g == 2 else 0
        for ct in range(4):
            proj_fm(k, ct, t0, t1, fm_dst(KC[g], ct * 128, 0, r=RR[g], l0=l0), RR[g], 0)
    for g in range(3):
        k = load_wblock(5632 + g * 512)
        t0, t1 = colrange(*kr[g])
        row0 = 768 if g == 2 else 0
        proj_tm(k, 0, 512, t0, t1, VC[g], row0, 0, ("VC%d" % g, "w"))

    far = list(range(0, 6)) + list(range(26, 32))
    for j, e_t in enumerate(far):
        stage_a(e_t, j)
    k = load_wblock(4096 + 2 * 512)
    for ct in range(4):
        proj_fm(k, ct, 0, 6, fm_dst(KC[2], ct * 128, 0, r=16, l0=0), 16, 0)
        proj_fm(k, ct, 6, 12, fm_dst(KC[2], ct * 128, 0, r=16, l0=208), 16, 0)
    k = load_wblock(5632 + 2 * 512)
    proj_tm(k, 0, 512, 0, 6, VC[2], 0, 0, ("VC2", "w"))
    proj_tm(k, 0, 512, 6, 12, VC[2], 3328, 0, ("VC2", "w"))
    S.barrier()
    if stop == "P":
        return

    A.off = base_off
    NU = 2304
    wcolT = A.f32(4 * 31).rearrange("p (c j) -> p c j", j=31)
    cbias = A.f32(4)
    cgam = A.f32(4)
    cbet = A.f32(4)
    for cc in range(4):
        S.dma("sp", "cw", lambda e, cc=cc: e.dma_start(out=wcolT[:, cc, :],
                                                    in_=W["conv_w"].ap()[l][:, cc * 128:(cc + 1) * 128].rearrange("j p -> p j"),
                                                    allow_slow_non_contiguous=True), writes=["wcolT"])
    for nm, tsb, tdr in (("cb", cbias, W["conv_b"]), ("cg", cgam, W["cnorm_g"]), ("cbt", cbet, W["cnorm_b"])):
        S.dma("sp", "c" + nm, lambda e, tsb=tsb, tdr=tdr: e.dma_start(
            out=tsb, in_=tdr.ap()[l].rearrange("(c p) -> p c", p=128), allow_slow_non_contiguous=True), writes=[nm])
    diag = [A.bf16(31 * 128).rearrange("p (j n) -> p j n", n=128) for _ in range(2)]
    aT = [A.bf16(NU) for _ in range(2)]
    gT = [A.bf16(NU) for _ in range(2)]
    sgm = [A.f32(NU) for _ in range(2)]
    zb = [A.bf16(NU) for _ in range(2)]
    zc = A.f32(4 * T).rearrange("p (c t) -> p c t", t=T)
    zcb = [A.bf16(512) for _ in range(4)]
    sqb = [A.bf16(512) for _ in range(4)]
    mean = A.f32(512)
    msq = A.f32(512)
    var = A.f32(512)
    rstd = A.f32(512)
    tmpc = [A.f32(512) for _ in range(2)]
    for cc in range(4):
        i = cc % 2
        S.dma("sp", "aT%d" % i, lambda e, i=i, cc=cc: e.dma_start(out=aT[i], in_=UT.ap()[cc * 128:(cc + 1) * 128, :]),
              reads=[("UT", "w")], writes=[("aT", i)])
        S.dma("sp", "gT%d" % i, lambda e, i=i, cc=cc: e.dma_start(out=gT[i], in_=UT.ap()[512 + cc * 128:512 + (cc + 1) * 128, :]),
              reads=[("UT", "w")], writes=[("gT", i)])
        for j in range(31):
            S.op("pool", lambda e, i=i, j=j, cc=cc: e.tensor_scalar(out=diag[i][:, j, :], in0=B.identf,
                                                                  scalar1=wcolT[:, cc, j:j + 1], scalar2=None,
                                                                  op0=ALU.mult),
                 reads=["identf", "wcolT"], writes=[("diag", i)])
        S.op("act", lambda e, i=i: e.activation(out=sgm[i], in_=gT[i], func=AF.Sigmoid), reads=[("gT", i)], writes=[("sgm", i)])
        S.op("dve", lambda e, i=i: e.tensor_tensor(out=zb[i], in0=aT[i], in1=sgm[i], op=ALU.mult),
             reads=[("aT", i), ("sgm", i)], writes=[("zb", i)])
        for G in range(4):
            b = B.bank()
            ps = B.banks[b]
            for j in range(31):
                o = 128 + G * 512 + j - 15
                B.mm(ps[:, :], diag[i][:, j, :], zb[i][:, o:o + 512], j == 0, j == 30,
                     [("diag", i), ("zb", i)], [("ps", b)], j == 30)
            S.op("act", lambda e, ps=ps, cc=cc, G=G: e.activation(out=zc[:, cc, G * 512:(G + 1) * 512], in_=ps[:, :],
                                                              func=AF.Identity, bias=cbias[:, cc:cc + 1]),
                 reads=[("ps", b), "cb"], writes=[("zc", cc, G)])
    if stop == "M1a":
        S.barrier()
        return
    for G in range(4):
        b1 = B.bank()
        b2 = B.bank()
        p1, p2 = B.banks[b1], B.banks[b2]
        for cc in range(4):
            i = cc
            S.op("act", lambda e, i=i, cc=cc, G=G: e.activation(out=zcb[i], in_=zc[:, cc, G * 512:(G + 1) * 512], func=AF.Copy),
                 reads=[("zc", cc, G)], writes=[("zcb", i)])
            B.mm(p1[:, :], B.ones, zcb[i], cc == 0, cc == 3, ["ones", ("zcb", i)], [("ps", b1)], cc == 3)
            S.op("act", lambda e, i=i, cc=cc, G=G: e.activation(out=sqb[i], in_=zc[:, cc, G * 512:(G + 1) * 512], func=AF.Square),
                 reads=[("zc", cc, G)], writes=[("sqb", i)])
            B.mm(p2[:, :], B.ones, sqb[i], cc == 0, cc == 3, ["ones", ("sqb", i)], [("ps", b2)], cc == 3)
        S.op("dve", lambda e, p1=p1: e.tensor_scalar(out=mean, in0=p1[:, :], scalar1=1.0 / 512, scalar2=None, op0=ALU.mult),
             reads=[("ps", b1)], writes=["mean"])
        S.op("dve", lambda e: e.tensor_tensor(out=msq, in0=mean, in1=mean, op=ALU.mult), reads=["mean"], writes=["msq"])
        S.op("dve", lambda e, p2=p2: e.scalar_tensor_tensor(out=var, in0=p2[:, :], scalar=1.0 / 512, in1=msq,
                                                          op0=ALU.mult, op1=ALU.subtract),
             reads=[("ps", b2), "msq"], writes=["var"])
        S.op("act", lambda e: e.activation(out=var, in_=var, func=AF.Sqrt, bias=EPS), reads=["var"], writes=["var"])
        S.op("dve", lambda e: e.reciprocal(out=rstd, in_=var), reads=["var"], writes=["rstd"])
        if stop == "M1b":
            continue
        for cc in range(4):
            i = cc % 2
            S.op("dve", lambda e, i=i, cc=cc, G=G: e.tensor_tensor(out=tmpc[i], in0=zc[:, cc, G * 512:(G + 1) * 512], in1=mean,
                                                               op=ALU.subtract),
                 reads=[("zc", cc, G), "mean"], writes=[("tmpc", i)])
            S.op("dve", lambda e, i=i: e.tensor_tensor(out=tmpc[i], in0=tmpc[i], in1=rstd, op=ALU.mult),
                 reads=[("tmpc", i), "rstd"], writes=[("tmpc", i)])
            S.op("act", lambda e, i=i, cc=cc, G=G: e.activation(
                out=zc[:, cc, G * 512:(G + 1) * 512].bitcast(BF16)[:, 0:512], in_=tmpc[i], func=AF.Silu,
                scale=cgam[:, cc:cc + 1], bias=cbet[:, cc:cc + 1]),
                reads=[("tmpc", i), "cg", "cbt"], writes=[("zc", cc, G)])
            S.dma("sp", "cvo%d" % (cc % 2), lambda e, cc=cc, G=G: e.dma_start(
                out=MIX.ap()[cc * 128:(cc + 1) * 128, G * 512:(G + 1) * 512],
                in_=zc[:, cc, G * 512:(G + 1) * 512].bitcast(BF16)[:, 0:512]),
                reads=[("zc", cc, G)], writes=[("MIX", "w")])
    S.barrier()
    if stop in ("M1", "M1b"):
        return

    A.off = base_off
    biasB = A.f32(3072).rearrange("p (j c n) -> p j c n", j=2, c=3)
    S.dma("sp", "bB", lambda e: e.dma_start(out=biasB, in_=W["biasB"].ap().rearrange("p (j c n) -> p j c n", j=2, c=3)),
          writes=["biasB"])
    esk = A.f32(8)
    S.dma("sp", "esk", lambda e: e.dma_start(out=esk, in_=bcast_row(W["sink"], l * 8, 8)), writes=["esk"])
    S.op("act", lambda e: e.activation(out=esk, in_=esk, func=AF.Exp), reads=["esk"], writes=["esk"])
    esB = A.f32(2 * 512).rearrange("p (j g q) -> p j g q", j=2, g=4)
    for j in range(2):
        for g in range(4):
            S.op("dve", lambda e, j=j, g=g: e.tensor_scalar(out=esB[:, j, g, :], in0=B.zerosf,
                                                           scalar1=esk[:, 4 * j + g:4 * j + g + 1], scalar2=None, op0=ALU.add),
                 reads=["zerosf", "esk"], writes=["esB"])
    KTb = [A.bf16(NU) for _ in range(2)]
    Vb = [A.bf16(18 * 128).rearrange("p (c d) -> p c d", d=128) for _ in range(2)]
    QTb = [A.bf16(4 * T).rearrange("p (g t) -> p g t", g=4) for _ in range(2)]
    obT = [A.bf16(4 * T).rearrange("p (g t) -> p g t", g=4) for _ in range(2)]
    tmpS = [A.f32(512) for _ in range(3)]
    PT = [A.bf16(512) for _ in range(6)]
    dsb = [A.f32(512) for _ in range(2)]
    for j in range(2):
        S.dma("sp", "KTb%d" % j, lambda e, j=j: e.dma_start(out=KTb[j], in_=KB.ap()[j * 128:(j + 1) * 128, :]),
              reads=[("KB", "w")], writes=[("KTb", j)])
        S.dma("sp", "Vb%d" % j, lambda e, j=j: e.dma_start(
            out=Vb[j], in_=VB.ap()[:, j * 128:(j + 1) * 128].rearrange("(c p) d -> p c d", p=128)),
            reads=[("VB", "w")], writes=[("Vb", j)])
        S.dma("sp", "QTb%d" % j, lambda e, j=j: e.dma_start(
            out=QTb[j], in_=QB.ap()[j * 512:(j + 1) * 512, :].rearrange("(g p) t -> p g t", p=128)),
            reads=[("QB", "w")], writes=[("QTb", j)])
    pcount = 0
    for j in range(2):
        for t in range(16):
            bd = B.bank()
            bo = B.bank()
            pts = []
            for c in range(3):
                bs = B.bank()
                ps = B.banks[bs]
                B.mm(ps[:, :], KTb[j][:, (t + c) * 128:(t + c + 1) * 128], QTb[j][:, :, t * 128:(t + 1) * 128],
                     True, True, [("KTb", j), ("QTb", j)], [("ps", bs)], True)
                ts_ = tmpS[c]
                S.op("dve", lambda e, ps=ps, ts_=ts_, c=c, j=j: e.scalar_tensor_tensor(
                    out=ts_, in0=ps[:, :], scalar=SCALE, in1=biasB[:, j, c, :], op0=ALU.mult, op1=ALU.add),
                    reads=[("ps", bs), "biasB"], writes=[("tmpS", c)])
                pi = pcount % 6
                pcount += 1
                pt = PT[pi]
                if t == 0 and c == 0:
                    bias_ap = W["kmask"][:, 0:1]
                elif t == 15 and c == 2:
                    bias_ap = W["kmask"][:, 1:2]
                else:
                    bias_ap = B.zcol
                S.op("act", lambda e, pt=pt, ts_=ts_, bias_ap=bias_ap: e.activation(out=pt, in_=ts_, func=AF.Exp, bias=bias_ap),
                     reads=[("tmpS", c), "kmask", "zcol"], writes=[("PT", pi)])
                pts.append((pt, pi))
            for c in range(3):
                pt, pi = pts[c]
                B.mm(B.banks[bd][:, :], B.ones, pt, c == 0, c == 2, ["ones", ("PT", pi)], [("ps", bd)], c == 2)
            for c in range(3):
                pt, pi = pts[c]
                B.mm(B.banks[bo][:, :], Vb[j][:, t + c, :], pt, c == 0, c == 2, [("Vb", j), ("PT", pi)], [("ps", bo)], c == 2)
            di = t % 2
            S.op("dve", lambda e, di=di, bd=bd, j=j: e.tensor_tensor(out=dsb[di], in0=B.banks[bd][:, :],
                                                                 in1=esB[:, j].rearrange("p g q -> p (g q)"), op=ALU.add),
                 reads=[("ps", bd), "esB"], writes=[("dsb", di)])
            S.op("dve", lambda e, di=di: e.reciprocal(out=dsb[di], in_=dsb[di]), reads=[("dsb", di)], writes=[("dsb", di)])
            S.op("dve", lambda e, di=di, bo=bo, j=j, t=t: e.tensor_tensor(
                out=obT[j][:, :, t * 128:(t + 1) * 128], in0=B.banks[bo][:, :].rearrange("p (g q) -> p g q", g=4),
                in1=dsb[di].rearrange("p (g q) -> p g q", g=4), op=ALU.mult),
                reads=[("ps", bo), ("dsb", di)], writes=[("obT", j)])
        S.dma("sp", "obT%d" % j, lambda e, j=j: e.dma_start(
            out=MIX.ap()[512 + j * 512:512 + (j + 1) * 512, :].rearrange("(g p) t -> p g t", p=128), in_=obT[j]),
            reads=[("obT", j)], writes=[("MIX", "w")])
    S.barrier()
    if stop == "M2":
        return

    A.off = base_off
    biasC = A.f32(3072).rearrange("p (h c q) -> p h c q", h=12, c=2)
    S.dma("sp", "bC", lambda e: e.dma_start(out=biasC, in_=W["biasC"].ap().rearrange("p (h c q) -> p h c q", h=12, c=2)),
          writes=["biasC"])
    accO = A.f32(4 * T).rearrange("p (h t) -> p h t", h=4)
    accD = A.f32(4 * T).rearrange("p (h t) -> p h t", h=4)
    Vt = A.bf16(32 * 512)
    KTc = [A.bf16(4096) for _ in range(2)]
    QTc = [A.bf16(T) for _ in range(2)]
    tmpC = [A.f32(256) for _ in range(2)]
    PC = [A.bf16(256) for _ in range(4)]
    cst = [A.bf16(T) for _ in range(2)]
    OFF = [64, 0, 0]
    NCH = [17, 5, 2]
    hc_count = 0
    pc_count = 0
    for g in range(3):
        r = RR[g]
        Lk = KCL[g] // r
        nch = NCH[g]
        Vtv = Vt[:, 0:r * nch * 512].rearrange("p (r c n) -> p r c n", r=r, c=nch)
        for rho in range(r):
            src = bass.AP(VC[g], (r * OFF[g] + rho) * 512, [[r * 512, 128], [128 * r * 512, nch], [1, 512]])
            S.dma("sp", "Vt", lambda e, rho=rho, src=src, Vtv=Vtv: e.dma_start(out=Vtv[:, rho], in_=src),
                  reads=[("VC%d" % g, "w")], writes=["Vt"])
        for h in range(4):
            ki = hc_count % 2
            hc_count += 1
            head = g * 4 + h
            KTv = KTc[ki][:, 0:KCL[g]].rearrange("p (r l) -> p r l", r=r)
            QTv = QTc[ki][:, :].rearrange("p (r l) -> p r l", r=r)
            S.dma("sp", "KTc%d" % ki, lambda e, ki=ki, g=g, h=h: e.dma_start(
                out=KTc[ki][:, 0:KCL[g]], in_=KC[g].ap()[h * 128:(h + 1) * 128, :]),
                reads=[(KC[g].name, "w")], writes=[("KTc", ki)])
            S.dma("sp", "QTc%d" % ki, lambda e, ki=ki, g=g, h=h: e.dma_start(
                out=QTc[ki], in_=QC[g].ap()[h * 128:(h + 1) * 128, :]),
                reads=[(QC[g].name, "w")], writes=[("QTc", ki)])
            ntile = 16 // r
            combos = [(rho, jt) for rho in range(r) for jt in range(ntile)]
            for cb0 in range(0, 16, 4):
                bd = B.bank()
                bo = B.bank()
                for ci in range(4):
                    rho, jt = combos[cb0 + ci]
                    bs = B.bank()
                    ps = B.banks[bs]
                    la = OFF[g] + 128 * jt
                    for c in range(2):
                        B.mm(ps[:, c * 128:(c + 1) * 128], KTv[:, rho, la + c * 128:la + (c + 1) * 128],
                             QTv[:, rho, jt * 128:(jt + 1) * 128], True, True,
                             [("KTc", ki), ("QTc", ki)], [("ps", bs)], c == 1)
                    ti = pc_count % 2
                    S.op("dve", lambda e, ps=ps, ti=ti, head=head: e.scalar_tensor_tensor(
                        out=tmpC[ti], in0=ps[:, 0:256], scalar=SCALE, in1=biasC[:, head].rearrange("p c q -> p (c q)"),
                        op0=ALU.mult, op1=ALU.add), reads=[("ps", bs), "biasC"], writes=[("tmpC", ti)])
                    pi = pc_count % 4
                    pc_count += 1
                    pc = PC[pi]
                    first = (jt == 0)
                    last = (jt == ntile - 1)
                    for c in range(2):
                        if c == 0 and first:
                            bias_ap = W["kmask"][:, 2:3]
                        elif c == 1 and last:
                            bias_ap = W["kmask"][:, 3:4]
                        else:
                            bias_ap = B.zcol
                        S.op("act", lambda e, pc=pc, ti=ti, c=c, bias_ap=bias_ap: e.activation(
                            out=pc[:, c * 128:(c + 1) * 128], in_=tmpC[ti][:, c * 128:(c + 1) * 128], func=AF.Exp, bias=bias_ap),
                            reads=[("tmpC", ti), "kmask", "zcol"], writes=[("PC", pi, c)])
                    for c in range(2):
                        B.mm(B.banks[bd][:, ci * 128:(ci + 1) * 128], B.ones, pc[:, c * 128:(c + 1) * 128], c == 0, c == 1,
                             ["ones", ("PC", pi, c)], [("ps", bd)], c == 1)
                    for c in range(2):
                        B.mm(B.banks[bo][:, ci * 128:(ci + 1) * 128], Vtv[:, rho, jt + c, h * 128:(h + 1) * 128],
                             pc[:, c * 128:(c + 1) * 128], c == 0, c == 1, ["Vt", ("PC", pi, c)], [("ps", bo)], c == 1)
                if g == 0:
                    t0_ = cb0 * 128
                    dO = accO[:, h, t0_:t0_ + 512]
                    dD = accD[:, h, t0_:t0_ + 512]
                    sO = B.banks[bo][:, :]
                    sD = B.banks[bd][:, :]
                elif g == 1:
                    rho = combos[cb0][0]
                    dO = accO[:, h, :].rearrange("p (l r) -> p r l", r=4)[:, rho, :]
                    dD = accD[:, h, :].rearrange("p (l r) -> p r l", r=4)[:, rho, :]
                    sO = B.banks[bo][:, :]
                    sD = B.banks[bd][:, :]
                else:
                    rho0 = combos[cb0][0]
                    dO = accO[:, h, :].rearrange("p (l r) -> p r l", r=16)[:, rho0:rho0 + 4, :]
                    dD = accD[:, h, :].rearrange("p (l r) -> p r l", r=16)[:, rho0:rho0 + 4, :]
                    sO = B.banks[bo][:, :].rearrange("p (r l) -> p r l", r=4)
                    sD = B.banks[bd][:, :].rearrange("p (r l) -> p r l", r=4)
                if g == 0:
                    S.op("dve", lambda e, dO=dO, sO=sO: e.tensor_copy(out=dO, in_=sO), reads=[("ps", bo)], writes=[("accO", h)])
                    S.op("act", lambda e, dD=dD, sD=sD: e.activation(out=dD, in_=sD, func=AF.Copy), reads=[("ps", bd)], writes=[("accD", h)])
                else:
                    S.op("dve", lambda e, dO=dO, sO=sO: e.tensor_tensor(out=dO, in0=sO, in1=dO, op=ALU.add),
                         reads=[("ps", bo), ("accO", h)], writes=[("accO", h)])
                    S.op("dve", lambda e, dD=dD, sD=sD: e.tensor_tensor(out=dD, in0=sD, in1=dD, op=ALU.add),
                         reads=[("ps", bd), ("accD", h)], writes=[("accD", h)])
    for h in range(4):
        i = h % 2
        S.op("dve", lambda e, h=h: e.reciprocal(out=accD[:, h, :], in_=accD[:, h, :]), reads=[("accD", h)], writes=[("accD", h)])
        S.op("dve", lambda e, h=h, i=i: e.tensor_tensor(out=cst[i], in0=accO[:, h, :], in1=accD[:, h, :], op=ALU.mult),
             reads=[("accO", h), ("accD", h)], writes=[("cst", i)])
        S.dma("sp", "cst%d" % i, lambda e, h=h, i=i: e.dma_start(out=MIX.ap()[1536 + h * 128:1536 + (h + 1) * 128, :], in_=cst[i]),
              reads=[("cst", i)], writes=[("MIX", "w")])
    S.barrier()
    if stop == "M3":
        return

    A.off = base_off
    g2 = A.f32(D)
    S.dma("sp", "g2", lambda e: e.dma_start(out=g2, in_=bcast_row(W["ln2_g"], l * D, D)), writes=["g2"])
    if final:
        gf = A.f32(D)
        S.dma("sp", "gf", lambda e: e.dma_start(out=gf, in_=bcast_row(W["lnf_g"], 0, D)), writes=["gf"])
    xgl = [A.f32(D) for _ in range(4)]
    NUN = 60
    act = A.bf16(NUN * 512).rearrange("p (u t) -> p u t", t=512)
    NWF = 8
    wf = [A.bf16(16 * 256).rearrange("p (a n) -> p a n", n=256) for _ in range(NWF)]
    hb2raw = A.f32(D)
    hb2 = [hb2raw[:, 0:1024].bitcast(BF16), hb2raw[:, 1024:2048].bitcast(BF16)]
    junk2 = A.bf16(D)
    ss2 = A.f32(8)
    sg = [A.f32(512) for _ in range(3)]
    tA = A.f32(512)
    tB = A.f32(512)
    sl_ = [A.f32(512) for _ in range(2)]
    fo = hb2raw

    def wjobs():
        jobs = []
        for c2 in range(8):
            cs = slice(c2 * 256, (c2 + 1) * 256)
            for br in range(3):
                jobs.append([(W["w_in"].ap()[l, :, 7168 + br * D + c2 * 256:7168 + br * D + (c2 + 1) * 256], 0, 16)])
            jobs.append([(W["w_a"].ap()[l, :, cs], 0, 4), (W["w_b"].ap()[l, :, cs], 4, 8), (W["w_c"].ap()[l, :, cs], 12, 4)])
        for c2 in range(8):
            jobs.append([(W["w_o"].ap()[l, :, c2 * 256:(c2 + 1) * 256], 0, 16)])
        for f2 in range(22):
            jobs.append([(W["w_ffn_in"].ap()[l, :, f2 * 256:(f2 + 1) * 256], 0, 16)])
            jobs.append([(W["w_ffn_in"].ap()[l, :, DFF + f2 * 256:DFF + (f2 + 1) * 256], 0, 16)])
        for c2 in range(8):
            for kq in range(3):
                k0 = kq * 16
                nk = min(16, 44 - k0)
                jobs.append([(W["w_ffn_out"].ap()[l, k0 * 128:(k0 + nk) * 128, c2 * 256:(c2 + 1) * 256], 0, nk)])
        return jobs

    jobs = wjobs()
    NJ = len(jobs)
    wstate = {"issued": 0, "used": 0}

    def issue_w(upto):
        while wstate["issued"] < min(upto, 4 * NJ):
            n = wstate["issued"]
            k = n % NWF
            for (src, kc0, nk) in jobs[n % NJ]:
                srcv = src.rearrange("(kc p) n -> p kc n", p=128)
                S.dma("pool", "wF%d" % k, lambda e, k=k, srcv=srcv, kc0=kc0, nk=nk: e.dma_start(out=wf[k][:, kc0:kc0 + nk, :], in_=srcv),
                      writes=[("wf", k)])
            wstate["issued"] += 1

    def reserve_w(kn):
        n = wstate["used"]
        issue_w(n + kn)
        wstate["used"] += kn
        return [(n + i) % NWF for i in range(kn)]

    def prefetch_w():
        issue_w(wstate["used"] + NWF)

    for G in range(4):
        tok0 = G * 512
        for a in range(4):
            S.dma("sp", "xg%d" % a, lambda e, a=a, tok0=tok0: e.dma_start(
                out=xgl[a], in_=x_own.ap()[tok0 + a * 128:tok0 + (a + 1) * 128, :]), writes=[("xg", a)])
        S.dma("sp", "hTg", lambda e, tok0=tok0: e.dma_start(
            out=act[:, 0:16, :], in_=HT.ap()[:, tok0:tok0 + 512].rearrange("(kc p) t -> p kc t", p=128)),
            reads=[("HT", "w")], writes=[("act", u) for u in range(0, 16)])
        S.dma("sp", "mxg", lambda e, tok0=tok0: e.dma_start(
            out=act[:, 16:32, :], in_=MIX.ap()[:, tok0:tok0 + 512].rearrange("(kc p) t -> p kc t", p=128)),
            reads=[("MIX", "w")], writes=[("act", u) for u in range(16, 32)])
        for c2 in range(8):
            kw = reserve_w(4)
            kg, ky = kw[0:3], kw[3]
            for cq in range(2):
                c = c2 * 2 + cq
                for br in range(3):
                    b = B.bank()
                    for kc in range(16):
                        B.mm(B.banks[b][:, :], wf[kg[br]][:, kc, cq * 128:(cq + 1) * 128], act[:, kc, :], kc == 0, kc == 15,
                             [("wf", kg[br]), ("act", kc)], [("ps", b)], kc == 15)
                    S.op("act", lambda e, b=b, br=br: e.activation(out=sg[br], in_=B.banks[b][:, :], func=AF.Sigmoid),
                         reads=[("ps", b)], writes=[("sg", br)])
                yb = []
                for (k0, nk, u0) in ((0, 4, 16), (4, 8, 20), (12, 4, 28)):
                    b = B.bank()
                    for kk in range(nk):
                        B.mm(B.banks[b][:, :], wf[ky][:, k0 + kk, cq * 128:(cq + 1) * 128], act[:, u0 + kk, :], kk == 0, kk == nk - 1,
                             [("wf", ky), ("act", u0 + kk)], [("ps", b)], kk == nk - 1)
                    yb.append(b)
                S.op("dve", lambda e, b=yb[0]: e.tensor_tensor(out=tA, in0=B.banks[b][:, :], in1=sg[0], op=ALU.mult),
                     reads=[("ps", yb[0]), ("sg", 0)], writes=["tA"])
                S.op("dve", lambda e, b=yb[1]: e.tensor_tensor(out=tB, in0=B.banks[b][:, :], in1=sg[1], op=ALU.mult),
                     reads=[("ps", yb[1]), ("sg", 1)], writes=["tB"])
                S.op("dve", lambda e: e.tensor_tensor(out=tA, in0=tA, in1=tB, op=ALU.add), reads=["tA", "tB"], writes=["tA"])
                S.op("dve", lambda e, b=yb[2]: e.tensor_tensor(out=tB, in0=B.banks[b][:, :], in1=sg[2], op=ALU.mult),
                     reads=[("ps", yb[2]), ("sg", 2)], writes=["tB"])
                S.op("dve", lambda e, c=c: e.tensor_tensor(out=act[:, 32 + c, :], in0=tA, in1=tB, op=ALU.add),
                     reads=["tA", "tB"], writes=[("act", 32 + c)])
            prefetch_w()
        for c2 in range(8):
            k = reserve_w(1)[0]
            for a in range(4):
                b = B.bank()
                for kc in range(16):
                    B.mm(B.banks[b][:, 0:256], act[:, 32 + kc, a * 128:(a + 1) * 128], wf[k][:, kc, :], kc == 0, kc == 15,
                         [("wf", k), ("act", 32 + kc)], [("ps", b)], kc == 15)
                S.op("dve", lambda e, a=a, b=b, c2=c2: e.tensor_tensor(out=xgl[a][:, c2 * 256:(c2 + 1) * 256], in0=B.banks[b][:, 0:256],
                                                                   in1=xgl[a][:, c2 * 256:(c2 + 1) * 256], op=ALU.add),
                     reads=[("ps", b), ("xg", a)], writes=[("xg", a)])
            prefetch_w()
        for a in range(4):
            i = a % 2
            B.norm_tile(xgl[a], ("xg", a), g2, "g2", hb2[i], ("hb2", i), junk2, ss2[:, 4 * i:4 * i + 4], "F%d" % i)
            B.transpose_tile(hb2[i], ("hb2", i),
                             lambda half, a=a: act[:, half * 8:(half + 1) * 8, a * 128:(a + 1) * 128],
                             [("act", u) for u in range(16)], a)
        for f2 in range(22):
            kgw, kuw = reserve_w(2)
            for fq in range(2):
                f = f2 * 2 + fq
                bg = B.bank()
                for kc in range(16):
                    B.mm(B.banks[bg][:, :], wf[kgw][:, kc, fq * 128:(fq + 1) * 128], act[:, kc, :], kc == 0, kc == 15,
                         [("wf", kgw), ("act", kc)], [("ps", bg)], kc == 15)
                bu = B.bank()
                for kc in range(16):
                    B.mm(B.banks[bu][:, :], wf[kuw][:, kc, fq * 128:(fq + 1) * 128], act[:, kc, :], kc == 0, kc == 15,
                         [("wf", kuw), ("act", kc)], [("ps", bu)], kc == 15)
                si = f % 2
                S.op("act", lambda e, bg=bg, si=si: e.activation(out=sl_[si], in_=B.banks[bg][:, :], func=AF.Silu),
                     reads=[("ps", bg)], writes=[("sl", si)])
                S.op("dve", lambda e, bu=bu, si=si, f=f: e.tensor_tensor(out=act[:, 16 + f, :], in0=B.banks[bu][:, :], in1=sl_[si], op=ALU.mult),
                     reads=[("ps", bu), ("sl", si)], writes=[("act", 16 + f)])
            prefetch_w()
        for c2 in range(8):
            ks = reserve_w(3)
            for a in range(4):
                b = B.bank()
                for f in range(44):
                    kq, kk = divmod(f, 16)
                    B.mm(B.banks[b][:, 0:256], act[:, 16 + f, a * 128:(a + 1) * 128], wf[ks[kq]][:, kk, :], f == 0, f == 43,
                         [("wf", ks[kq]), ("act", 16 + f)], [("ps", b)], f == 43)
                S.op("dve", lambda e, a=a, b=b, c2=c2: e.tensor_tensor(out=xgl[a][:, c2 * 256:(c2 + 1) * 256], in0=B.banks[b][:, 0:256],
                                                                   in1=xgl[a][:, c2 * 256:(c2 + 1) * 256], op=ALU.add),
                     reads=[("ps", b), ("xg", a)], writes=[("xg", a)])
            prefetch_w()
        for a in range(4):
            r0 = tok0 + a * 128
            if final:
                S.op("act", lambda e, a=a: e.activation(out=junk2, in_=xgl[a], func=AF.Square, accum_out=ss2[:, 0:1]),
                     reads=[("xg", a)], writes=["junkFF", "ssF"])
                S.op("act", lambda e: e.activation(out=ss2[:, 1:2], in_=ss2[:, 0:1], func=AF.Sqrt, scale=1.0 / D, bias=EPS),
                     reads=["ssF"], writes=["ssFb"])
                S.op("dve", lambda e: e.reciprocal(out=ss2[:, 2:3], in_=ss2[:, 1:2]), reads=["ssFb"], writes=["ssFc"])
                S.op("dve", lambda e, a=a: e.scalar_tensor_tensor(out=fo, in0=xgl[a], scalar=ss2[:, 2:3], in1=gf,
                                                                 op0=ALU.mult, op1=ALU.mult),
                     reads=[("xg", a), "ssFc", "gf"], writes=[("hb2", 0), ("hb2", 1)])
                S.dma("sp", "xo", lambda e, r0=r0: e.dma_start(out=x_dst.ap()[r0:r0 + 128, :], in_=fo),
                      reads=[("hb2", 0), ("hb2", 1)], writes=[("xdst", r0)])
            else:
                S.dma("sp", "xo%d" % a, lambda e, r0=r0, a=a: e.dma_start(out=x_dst.ap()[r0:r0 + 128, :], in_=xgl[a]),
                      reads=[("xg", a)], writes=[("xdst", r0)])
    S.barrier()


def _tables():
    hs = np.arange(1, 21, dtype=np.float64)
    slopes = np.exp2(-8.0 * hs / 20.0)
    kp = np.arange(128)[:, None]
    q = np.arange(128)[None, :]
    bB = np.zeros((128, 2, 3, 4, 128), np.float64)
    for j in range(2):
        for g in range(4):
            s = slopes[4 * j + g]
            rel = kp - 128 - q
            bB[:, j, 0, g, :] = np.where(kp >= q, -s * np.abs(rel), NEGM)
            rel = kp - q
            bB[:, j, 1, g, :] = -s * np.abs(rel)
            rel = kp + 128 - q
            bB[:, j, 2, g, :] = np.where(kp <= q, -s * np.abs(rel), NEGM)
    bC = np.zeros((128, 12, 2, 128), np.float64)
    for g in range(3):
        r = (1, 4, 16)[g]
        for h in range(4):
            s = slopes[8 + g * 4 + h]
            for c, off in ((0, -64), (1, 64)):
                rel = kp + off - q
                bC[:, g * 4 + h, c, :] = np.where(np.abs(rel) <= 64, -s * r * np.abs(rel), NEGM)
    return (np.ascontiguousarray(bB.reshape(128, 3072).astype(np.float32)),
            np.ascontiguousarray(bC.reshape(128, 3072).astype(np.float32)))


def _core_consts(c):
    own0 = c * T
    idx = np.zeros((128, 32), np.int32)
    for e in range(32):
        tok = own0 - 1024 + 128 * e + np.arange(128)
        idx[:, e] = np.where((tok >= 0) & (tok < SEQ), tok, 1 << 24)
    km = np.zeros((128, 4), np.float32)
    if c == 0:
        km[:, 0] = NEGM
        km[0:64, 2] = NEGM
    if c == NCORES - 1:
        km[:, 1] = NEGM
        km[64:128, 3] = NEGM
    return idx, km


_PROGS = {}


def _get_prog(layer_ids, final):
    key = (tuple(layer_ids), final)
    if key not in _PROGS:
        _PROGS[key] = build_program(list(layer_ids), final)
    return _PROGS[key]


def kernel(x, ln1_g, w_in, conv_w, conv_b, cnorm_g, cnorm_b, w_a, sink, w_b, w_c, w_o, ln2_g, w_ffn_in,
           w_ffn_out, lnf_g):
    f = lambda a: np.ascontiguousarray(np.asarray(a, dtype=np.float32))
    bB, bC = _tables()
    shared = dict(biasB=bB, biasC=bC, ln1_g=f(ln1_g), w_in=f(w_in), conv_w=f(conv_w), conv_b=f(conv_b),
                  cnorm_g=f(cnorm_g), cnorm_b=f(cnorm_b), w_a=f(w_a), sink=f(sink), w_b=f(w_b), w_c=f(w_c),
                  w_o=f(w_o), ln2_g=f(ln2_g), w_ffn_in=f(w_ffn_in), w_ffn_out=f(w_ffn_out), lnf_g=f(lnf_g))
    cc = [_core_consts(c) for c in range(NCORES)]
    xcur = f(x).reshape(SEQ, D)
    for l in range(2):
        nc = _get_prog([l], l == 1)
        xpad = np.concatenate([np.zeros((1024, D), np.float32), xcur, np.zeros((1024, D), np.float32)], axis=0)
        in_maps = []
        for c in range(NCORES):
            own0 = c * T
            halo = np.ascontiguousarray(np.concatenate([xpad[own0:own0 + 1024], xpad[own0 + 1024 + T:own0 + 2048 + T]], axis=0))
            in_maps.append(dict(shared, xown=np.ascontiguousarray(xcur[own0:own0 + T]), xhalo=halo, idx=cc[c][0], kmask=cc[c][1]))
        res = run_bass_kernel_spmd(nc, in_maps, core_ids=list(range(NCORES)))
        xcur = np.ascontiguousarray(np.concatenate([np.asarray(r["xout"]) for r in res.results], axis=0))
    return xcur.reshape(1, SEQ, D).astype(np.float32)
```
